# Optimizing a Trainium2 kernel written in Bass

```python
import math
import jax, jax.numpy as jnp
from jax import lax
import numpy as np

D_MODEL = 4096
BATCH = 4
SEQ = 2048
DEPTH = 1
DEC_BATCH = 128
DEC_SEQ = 8
PAST_LEN = 16384
PAGE_SIZE = 128

POOL_WIDTH = D_MODEL // 4
POOL_WINDOWS = (2, 4, 8, 16)
POOL_GROUPS = len(POOL_WINDOWS)
POOL_GROUP_WIDTH = POOL_WIDTH // POOL_GROUPS
POOL_BUF = max(POOL_WINDOWS) - 1
DN_WIDTH = D_MODEL - POOL_WIDTH
DN_HEAD_DIM = 128
DN_HEADS = DN_WIDTH // DN_HEAD_DIM
CONV_WIDTH = 4
CONV_CH = 3 * DN_WIDTH
PROJ_OUT = POOL_WIDTH + 4 * DN_WIDTH + 2 * DN_HEADS
D_FF = -(-8 * D_MODEL // (3 * 256)) * 256
PLE_DIM = 256
CHUNK = 64
LN_EPS = 1e-5
RMS_EPS = 1e-6
L2_EPS = 1e-6
DN_ALPHA = (2.0 * DEPTH) ** 0.25
DN_BETA = (8.0 * DEPTH) ** -0.25

kernel_name = 'hymba_pool_gdn_deepnorm_step'


def _layer_norm(x, g, b):
    xf = x.astype(jnp.float32)
    mu = jnp.mean(xf, -1, keepdims=True)
    var = jnp.mean(jnp.square(xf - mu), -1, keepdims=True)
    y = (xf - mu) * lax.rsqrt(var + LN_EPS) * g.astype(jnp.float32) + b.astype(jnp.float32)
    return y.astype(x.dtype)


def _l2norm(x):
    return x * lax.rsqrt(jnp.sum(x * x, -1, keepdims=True) + L2_EPS)


def _pool_mixer(u, buf, start, w_pool, pool_scale):
    B, L, _ = u.shape
    ext = jnp.concatenate([buf.astype(u.dtype), u], axis=1)
    cs = jnp.cumsum(ext.astype(jnp.float32), axis=1)
    cs = jnp.concatenate([jnp.zeros((B, 1, POOL_WIDTH), jnp.float32), cs], axis=1)
    pos = start + jnp.arange(L)
    uf = u.astype(jnp.float32)
    outs = []
    for gi, w in enumerate(POOL_WINDOWS):
        lo, hi = gi * POOL_GROUP_WIDTH, (gi + 1) * POOL_GROUP_WIDTH
        s_end = cs[:, POOL_BUF + 1:, lo:hi]
        s_begin = cs[:, POOL_BUF + 1 - w:POOL_BUF + 1 - w + L, lo:hi]
        cnt = jnp.minimum(pos + 1, w).astype(jnp.float32)[None, :, None]
        outs.append((s_end - s_begin) / cnt - uf[..., lo:hi])
    d = jnp.stack(outs, axis=2).astype(u.dtype)
    y = jnp.einsum('blgc,gcd->blgd', d, w_pool).reshape(B, L, POOL_WIDTH) * pool_scale
    return y, ext[:, -POOL_BUF:]


def _short_conv(xc, buf, w_conv):
    L = xc.shape[1]
    ext = jnp.concatenate([buf.astype(xc.dtype), xc], axis=1)
    out = ext[:, 0:L] * w_conv[0]
    for j in range(1, CONV_WIDTH):
        out = out + ext[:, j:j + L] * w_conv[j]
    return jax.nn.silu(out), ext[:, -(CONV_WIDTH - 1):]


def _gated_delta(q, k, v, g, beta, s0):
    B, L, H, DK = q.shape
    DV = v.shape[-1]
    C = math.gcd(L, CHUNK)
    N = L // C

    def blk(t):
        return jnp.moveaxis(t.reshape((B, N, C, H) + t.shape[3:]), 3, 1)

    q, k, v, g, beta = blk(q), blk(k), blk(v), blk(g), blk(beta)
    gc = jnp.cumsum(g, axis=-1)
    incl = jnp.tril(jnp.ones((C, C), bool))
    strict = jnp.tril(jnp.ones((C, C), bool), -1)
    diff = gc[..., :, None] - gc[..., None, :]
    decay = jnp.where(incl, jnp.exp(jnp.where(incl, diff, 0.0)), 0.0)
    kb = k * beta[..., None]
    vb = v * beta[..., None]
    a_mat = jnp.where(strict, jnp.einsum('bhnik,bhnjk->bhnij', kb, k) * decay, 0.0)
    eye = jnp.eye(C, dtype=jnp.float32)
    t_mat = lax.linalg.triangular_solve(a_mat + eye, jnp.broadcast_to(eye, a_mat.shape),
                                        left_side=True, lower=True)
    u_intra = jnp.einsum('bhnij,bhnjv->bhniv', t_mat, vb)
    w_intra = jnp.einsum('bhnij,bhnjk->bhnik', t_mat, kb * jnp.exp(gc)[..., None])
    qk = jnp.where(incl, jnp.einsum('bhnik,bhnjk->bhnij', q, k) * decay, 0.0)

    def step(s, xs):
        qn, kn, un, wn, gn, qkn = xs
        v_new = un - jnp.einsum('bhck,bhkv->bhcv', wn, s)
        o = (jnp.einsum('bhck,bhkv->bhcv', qn * jnp.exp(gn)[..., None], s)
             + jnp.einsum('bhij,bhjv->bhiv', qkn, v_new))
        g_last = gn[..., -1:]
        s = s * jnp.exp(g_last)[..., None] + jnp.einsum(
            'bhck,bhcv->bhkv', kn * jnp.exp(g_last - gn)[..., None], v_new)
        return s, o

    xs = tuple(jnp.moveaxis(t, 2, 0) for t in (q, k, u_intra, w_intra, gc, qk))
    s_fin, o = lax.scan(step, s0, xs)
    o = jnp.transpose(o, (1, 0, 3, 2, 4)).reshape(B, L, H, DV)
    return o, s_fin


def _mixer(x, pool_buf, conv_buf, s0, start, w_in, w_pool, pool_scale, w_conv, a_log, dt_bias,
           o_norm_g, w_out):
    B, L, _ = x.shape
    f32 = jnp.float32
    proj = jnp.einsum('bld,de->ble', x, w_in)
    o1 = POOL_WIDTH
    o2 = o1 + CONV_CH
    o3 = o2 + DN_WIDTH
    o4 = o3 + DN_HEADS
    u = proj[..., :o1]
    qkv = proj[..., o1:o2]
    z = proj[..., o2:o3]
    b_raw = proj[..., o3:o4]
    a_raw = proj[..., o4:]
    y_pool, new_pool = _pool_mixer(u, pool_buf, start, w_pool, pool_scale)
    qkv, new_conv = _short_conv(qkv, conv_buf, w_conv)
    qkv = qkv.astype(f32).reshape(B, L, 3, DN_HEADS, DN_HEAD_DIM)
    q = _l2norm(qkv[:, :, 0]) * DN_HEAD_DIM ** -0.5
    k = _l2norm(qkv[:, :, 1])
    v = qkv[:, :, 2]
    beta = jax.nn.sigmoid(b_raw.astype(f32))
    g = -jnp.exp(a_log.astype(f32)) * jax.nn.softplus(a_raw.astype(f32) + dt_bias.astype(f32))
    o, s_new = _gated_delta(q, k, v, g, beta, s0.astype(f32))
    o = o * lax.rsqrt(jnp.mean(o * o, -1, keepdims=True) + RMS_EPS) * o_norm_g.astype(f32)
    o = o * jax.nn.silu(z.astype(f32)).reshape(B, L, DN_HEADS, DN_HEAD_DIM)
    o = o.reshape(B, L, DN_WIDTH).astype(x.dtype)
    mixed = jnp.concatenate([y_pool.astype(x.dtype), o], axis=-1)
    return jnp.einsum('bld,de->ble', mixed, w_out), new_pool, new_conv, s_new


def _layer(x, p, pool_buf, conv_buf, s0, start, lw):
    (w_in, w_pool, pool_scale, w_conv, a_log, dt_bias, o_norm_g, w_out,
     ln1_g, ln1_b, w_gate_up, w_down, ln2_g, ln2_b, w_ple_gate, w_ple_proj) = lw
    mix, new_pool, new_conv, s_new = _mixer(x, pool_buf, conv_buf, s0, start, w_in, w_pool,
                                            pool_scale, w_conv, a_log, dt_bias, o_norm_g, w_out)
    h = _layer_norm(DN_ALPHA * x + mix, ln1_g, ln1_b)
    gu = jnp.einsum('bld,df->blf', h, w_gate_up)
    ff = jnp.einsum('blf,fd->bld', jax.nn.silu(gu[..., :D_FF]) * gu[..., D_FF:], w_down)
    h = _layer_norm(DN_ALPHA * h + ff, ln2_g, ln2_b)
    gate = jax.nn.sigmoid(jnp.einsum('bld,de->ble', h, w_ple_gate).astype(jnp.float32))
    e = jnp.einsum('blp,pd->bld', p.astype(x.dtype), w_ple_proj).astype(jnp.float32)
    y = h + (gate * e).astype(h.dtype)
    return y, new_pool, new_conv, s_new


def setup_inputs(seed: int = 0) -> dict:
    key = jax.random.key(seed)
    ks = jax.random.split(key, 24)
    f32 = jnp.float32

    def nrm(k, shape, s):
        return jax.random.normal(k, shape, f32) * s

    G = POOL_GROUP_WIDTH
    x_prompt = nrm(ks[0], (BATCH, SEQ, D_MODEL), 1.0)
    x_sample = nrm(ks[1], (DEC_BATCH, DEC_SEQ, D_MODEL), 1.0)
    state_pool = nrm(ks[2], (DEPTH, DEC_BATCH, POOL_BUF, POOL_WIDTH), 1.0)
    state_conv = nrm(ks[3], (DEPTH, DEC_BATCH, CONV_WIDTH - 1, CONV_CH), 1.0)
    state_delta = nrm(ks[4], (DEPTH, DEC_BATCH, DN_HEADS, DN_HEAD_DIM, DN_HEAD_DIM), 0.05)
    p_prompt = nrm(ks[5], (DEPTH, BATCH, SEQ, PLE_DIM), 1.0)
    p_sample = nrm(ks[6], (DEPTH, DEC_BATCH, DEC_SEQ, PLE_DIM), 1.0)
    w_in = nrm(ks[7], (DEPTH, D_MODEL, PROJ_OUT), D_MODEL ** -0.5)
    w_pool = nrm(ks[8], (DEPTH, POOL_GROUPS, G, G), G ** -0.5)
    pool_scale = 1.0 + nrm(ks[9], (DEPTH, POOL_WIDTH), 0.1)
    w_conv = nrm(ks[10], (DEPTH, CONV_WIDTH, CONV_CH), CONV_WIDTH ** -0.5)
    a_log = jnp.log(jax.random.uniform(ks[11], (DEPTH, DN_HEADS), f32, 1.0, 16.0))
    dt = jnp.exp(jax.random.uniform(ks[12], (DEPTH, DN_HEADS), f32, math.log(1e-3), math.log(1e-1)))
    dt_bias = dt + jnp.log(-jnp.expm1(-dt))
    o_norm_g = 1.0 + nrm(ks[13], (DEPTH, DN_HEAD_DIM), 0.1)
    w_out = nrm(ks[14], (DEPTH, D_MODEL, D_MODEL), D_MODEL ** -0.5 * DN_BETA)
    ln1_g = 1.0 + nrm(ks[15], (DEPTH, D_MODEL), 0.1)
    ln1_b = nrm(ks[16], (DEPTH, D_MODEL), 0.02)
    w_gate_up = nrm(ks[17], (DEPTH, D_MODEL, 2 * D_FF), D_MODEL ** -0.5)
    w_down = nrm(ks[18], (DEPTH, D_FF, D_MODEL), D_FF ** -0.5 * DN_BETA)
    ln2_g = 1.0 + nrm(ks[19], (DEPTH, D_MODEL), 0.1)
    ln2_b = nrm(ks[20], (DEPTH, D_MODEL), 0.02)
    w_ple_gate = nrm(ks[21], (DEPTH, D_MODEL, D_MODEL), D_MODEL ** -0.5)
    w_ple_proj = nrm(ks[22], (DEPTH, PLE_DIM, D_MODEL), PLE_DIM ** -0.5 * 0.5)
    return {'x_prompt': x_prompt, 'x_sample': x_sample, 'state_pool': state_pool,
            'state_conv': state_conv, 'state_delta': state_delta, 'p_prompt': p_prompt,
            'p_sample': p_sample, 'w_in': w_in, 'w_pool': w_pool, 'pool_scale': pool_scale,
            'w_conv': w_conv, 'a_log': a_log, 'dt_bias': dt_bias, 'o_norm_g': o_norm_g,
            'w_out': w_out, 'ln1_g': ln1_g, 'ln1_b': ln1_b, 'w_gate_up': w_gate_up,
            'w_down': w_down, 'ln2_g': ln2_g, 'ln2_b': ln2_b, 'w_ple_gate': w_ple_gate,
            'w_ple_proj': w_ple_proj}


def reference(x_prompt, x_sample, state_pool, state_conv, state_delta, p_prompt, p_sample,
              w_in, w_pool, pool_scale, w_conv, a_log, dt_bias, o_norm_g, w_out,
              ln1_g, ln1_b, w_gate_up, w_down, ln2_g, ln2_b, w_ple_gate, w_ple_proj):
    B = x_prompt.shape[0]
    yp, ys = x_prompt, x_sample
    pool_p, conv_p, delta_p, pool_s, conv_s, delta_s = [], [], [], [], [], []
    for i in range(DEPTH):
        lw = (w_in[i], w_pool[i], pool_scale[i], w_conv[i], a_log[i], dt_bias[i], o_norm_g[i],
              w_out[i], ln1_g[i], ln1_b[i], w_gate_up[i], w_down[i], ln2_g[i], ln2_b[i],
              w_ple_gate[i], w_ple_proj[i])
        pool0 = jnp.zeros((B, POOL_BUF, POOL_WIDTH), x_prompt.dtype)
        conv0 = jnp.zeros((B, CONV_WIDTH - 1, CONV_CH), x_prompt.dtype)
        s0 = jnp.zeros((B, DN_HEADS, DN_HEAD_DIM, DN_HEAD_DIM), jnp.float32)
        yp, npl, ncv, nst = _layer(yp, p_prompt[i], pool0, conv0, s0, 0, lw)
        ys, spl, scv, sst = _layer(ys, p_sample[i], state_pool[i], state_conv[i], state_delta[i],
                                   PAST_LEN, lw)
        pool_p.append(npl)
        conv_p.append(ncv)
        delta_p.append(nst)
        pool_s.append(spl)
        conv_s.append(scv)
        delta_s.append(sst)
    return (yp, ys, jnp.stack(pool_p), jnp.stack(conv_p), jnp.stack(delta_p),
            jnp.stack(pool_s), jnp.stack(conv_s), jnp.stack(delta_s))
```

```python
import math
from contextlib import ExitStack

import numpy as np
import concourse.bass as bass
import concourse.mybir as mybir
from concourse.bass_utils import run_bass_kernel_spmd

F32 = mybir.dt.float32
BF16 = mybir.dt.bfloat16
ALU = mybir.AluOpType
AF = mybir.ActivationFunctionType


class Cfg:
    def __init__(self, D=4096, NH=24, DFF=11008, SEQ=2048, BATCH=4, DECB=128, DEPTH=1):
        self.D, self.NH, self.DFF, self.SEQ, self.BATCH, self.DECB = D, NH, DFF, SEQ, BATCH, DECB
        self.PW = 1024
        self.DN = NH * 128
        assert self.PW + self.DN == D
        self.KC = D // 128
        self.FC = DFF // 128
        self.HALF = SEQ // 2
        self.NT = self.HALF // 128
        self.NMAIN = self.HALF + 128
        self.MW = 16 + self.HALF + 128
        self.PLE = 256
        self.CONVCH = 3 * self.DN
        self.ALPHA = (2.0 * DEPTH) ** 0.25
        self.NG = (self.NT + 1) // 3
        assert (self.NT + 1) % 3 == 0
        self.SLAB = 8192
        assert self.KC * 256 <= self.SLAB


class Buf:
    __slots__ = ("name", "w", "r", "wx")

    def __init__(self, name):
        self.name = name
        self.w = None
        self.r = []
        self.wx = []


class Eng:
    def __init__(self, K, name, eng, is_pe=False):
        self.name = name
        self.eng = eng
        self.is_pe = is_pe
        self.sem = K.es.enter_context(K.nc.semaphore("s_" + name))
        self.count = 0
        self.seen = {}
        self.dsems = []
        self.dma_n = 0

    def wait(self, ev):
        if ev is None:
            return
        sem, val = ev
        if self.is_pe and sem is self.sem:
            return
        key = id(sem)
        if self.seen.get(key, 0) >= val:
            return
        self.eng.wait_ge(sem, val)
        self.seen[key] = val


class Kern:
    def __init__(self, nc, n_dma_sems=10):
        self.nc = nc
        self.es = ExitStack()
        self.pe = Eng(self, "pe", nc.tensor, is_pe=True)
        self.act = Eng(self, "act", nc.scalar)
        self.dve = Eng(self, "dve", nc.vector)
        self.pool = Eng(self, "pool", nc.gpsimd)
        self.sp = Eng(self, "sp", nc.sync)
        for q in (self.sp, self.pool):
            for i in range(n_dma_sems):
                q.dsems.append(self.es.enter_context(nc.semaphore(f"d_{q.name}{i}")))
        self.out_events = []
        self._pr = []
        self._pw = []
        self.nb = 0
        self.rr = 0

    def buf(self, name=None):
        self.nb += 1
        return Buf(name or f"b{self.nb}")

    def sb(self, name, shape, dt, es=None):
        self.nb += 1
        name = f"{name}_{self.nb}"
        t = (es or self.es).enter_context(self.nc.sbuf_tensor(name, shape, dt))
        return t, self.buf(name)

    def _deps(self, e, reads, writes):
        for b in reads:
            e.wait(b.w)
            for ev in b.wx:
                e.wait(ev)
        for b in writes:
            e.wait(b.w)
            for ev in b.wx:
                e.wait(ev)
            for ev in b.r:
                e.wait(ev)

    def _commit(self, ev, reads, writes):
        for b in reads:
            b.r.append(ev)
            if len(b.r) > 16:
                d = {}
                for s, v in b.r:
                    k = id(s)
                    if k not in d or d[k][1] < v:
                        d[k] = (s, v)
                b.r = list(d.values())
        for b in writes:
            b.w = ev
            b.r = []
            b.wx = []

    def op(self, e, fn, reads=(), writes=()):
        self._deps(e, reads, writes)
        ins = fn()
        e.count += 1
        ins.then_inc(e.sem, 1)
        self._commit((e.sem, e.count), reads, writes)

    def mm(self, fn, reads=(), writes=(), inc=True):
        e = self.pe
        self._deps(e, reads, writes)
        ins = fn()
        self._pr.extend(reads)
        self._pw.extend(writes)
        if inc:
            e.count += 1
            ins.then_inc(e.sem, 1)
            self._commit((e.sem, e.count), list({id(b): b for b in self._pr}.values()),
                         list({id(b): b for b in self._pw}.values()))
            self._pr = []
            self._pw = []

    def dma(self, q, out, in_, reads=(), writes=(), is_output=False):
        self._deps(q, reads, writes)
        k = q.dma_n
        q.dma_n += 1
        ns = len(q.dsems)
        sem = q.dsems[k % ns]
        val = 16 * (k // ns + 1)
        if val > 16:
            q.wait((sem, val - 16))
        q.eng.dma_start(out=out, in_=in_).then_inc(sem, 16)
        ev = (sem, val)
        self._commit(ev, reads, writes)
        if is_output:
            self.out_events.append(ev)
        return ev

    def barrier(self):
        engs = (self.pe, self.act, self.dve, self.pool, self.sp)
        evs = [(o.sem, o.count) for o in engs if o.count > 0]
        for q in (self.sp, self.pool):
            ns = len(q.dsems)
            for i in range(min(ns, q.dma_n)):
                k = ((q.dma_n - 1 - i) // ns) * ns + i if False else None
            for i, sem in enumerate(q.dsems):
                cnt = (q.dma_n - i + ns - 1) // ns if q.dma_n > i else 0
                if cnt > 0:
                    evs.append((sem, 16 * cnt))
        for e in engs:
            for ev in evs:
                if e.is_pe and ev[0] is e.sem:
                    continue
                if ev[0] is e.sem:
                    continue
                e.wait(ev)

    def finish(self):
        for ev in self.out_events:
            self.sp.wait(ev)
        self.es.close()


class SlabStream:
    def __init__(self, K, bufs, srcs):
        self.K = K
        self.bufs = bufs
        self.srcs = srcs
        self.i = 0
        self.loaded = 0
        for _ in range(min(len(bufs), len(srcs))):
            self._load()

    def _load(self):
        j = self.loaded
        t, b = self.bufs[j % len(self.bufs)]
        src, n = self.srcs[j]
        if n > 4096:
            hh = n // 2
            self.K.dma(self.K.pool, t[:, 0:hh], src[:, 0:hh], writes=[b])
            ev = self.K.dma(self.K.pool, t[:, hh:n], src[:, hh:n])
            b.wx.append(ev)
        else:
            self.K.dma(self.K.pool, t[:, 0:n], src, writes=[b])
        self.loaded += 1

    def next(self):
        if self.i > 0 and self.loaded < len(self.srcs):
            self._load()
        t, b = self.bufs[self.i % len(self.bufs)]
        self.i += 1
        return t, b


def split_groups(start, length, maxn=512, align=8):
    n = (length + maxn - 1) // maxn
    out = []
    c = start
    rem = length
    for i in range(n):
        left = n - i
        l = -(-rem // left)
        if left > 1:
            l = -(-l // align) * align
        l = min(l, rem, maxn)
        out.append((c, l))
        c += l
        rem -= l
    assert rem == 0
    return out


def build(cfg, dbg=False, stage=3):
    nc = bass.Bass("TRN2", target_bir_lowering=False)
    K = Kern(nc)
    D, NH, KC, FC, HALF, NT, NMAIN, MW, DN = cfg.D, cfg.NH, cfg.KC, cfg.FC, cfg.HALF, cfg.NT, cfg.NMAIN, cfg.MW, cfg.DN
    PWC = cfg.PW // 128
    NTOKALL = 2 * HALF + 128

    def din(name, shape, dt=F32):
        return nc.dram_tensor(name, list(shape), dt, kind="ExternalInput").ap()

    def dout(name, shape, dt=F32):
        return nc.dram_tensor(name, list(shape), dt, kind="ExternalOutput").ap()

    xs = din("xs", [NTOKALL, D])
    ps_in = din("ps", [NMAIN, 256])
    spool = din("spool", [16, 15, 1024])
    sconv = din("sconv", [16 * 3, 3, NH, 128])
    sdelta = din("sdelta", [16, NH, 128, 128])
    invc = din("invc", [128, 4, 16])
    cm_p = din("cm_p", [128, 5, 128])
    cm_s = din("cm_s", [128, 5, 128])
    c_id = din("c_id", [128, 128])
    c_bm = din("c_bm", [128, 16, 128])
    c_rm = din("c_rm", [128, 16])
    w_hd = [din(f"w_hd{h}", [4, 128, KC * 128]) for h in range(NH)]
    w_u = din("w_u", [8, 128, KC * 128])
    w_ba = din("w_ba", [128, KC * 2 * NH])
    w_o = [din(f"w_o{i}", [128, KC * 256]) for i in range(D // 256)]
    GUG = 8
    w_gu_t = [din(f"w_gu{i}", [min(GUG, FC - i * GUG), 128, KC * 256]) for i in range((FC + GUG - 1) // GUG)]
    w_gu = [w_gu_t[i // GUG][i % GUG] for i in range(FC)]
    w_dn = [din(f"w_dn{i}", [128, FC * 512]) for i in range(D // 512)]
    w_pg = [din(f"w_pg{i}", [128, KC * 256]) for i in range(D // 256)]
    w_pp = din("w_pp", [128, 2 * D])
    w_pl = din("w_pl", [128, 4 * 2 * 256])
    v_ps = din("v_ps", [128, PWC])
    v_wc = din("v_wc", [128, 3 * NH * 4])
    v_hb = din("v_hb", [128, 2 * NH])
    v_on = din("v_on", [128, 1])
    v_ln = din("v_ln", [4, D])

    yo = dout("yo", [NMAIN, D])
    o_npp = dout("o_npp", [15, 1024])
    o_ncp = dout("o_ncp", [3, 3, NH, 128])
    o_ndp = dout("o_ndp", [NH, 128, 128])
    o_nps = dout("o_nps", [16, 15, 1024])
    o_ncs = dout("o_ncs", [16 * 3, 3, NH, 128])
    o_nds = dout("o_nds", [16, NH, 128, 128])
    skind = "ExternalOutput" if dbg else "Internal"
    mixd = nc.dram_tensor("mixd", [KC, 128, NMAIN], BF16, kind=skind).ap()
    hsd = nc.dram_tensor("hsd", [NMAIN, D], F32, kind=skind).ap()
    npsd = nc.dram_tensor("npsd", [128, 1024], F32, kind="Internal").ap()
    npsdB = K.buf()
    mixB = [K.buf() for _ in range(KC)]
    hsB = [K.buf() for _ in range(NT + 1)]

    pe, act, dve, pool, sp = K.pe, K.act, K.dve, K.pool, K.sp
    V, A, T = nc.vector, nc.scalar, nc.tensor

    ident_f, identfB = K.sb("ident_f", [128, 128], F32)
    ident_b, identbB = K.sb("ident_b", [128, 128], BF16)
    ones_b, onesB = K.sb("ones_b", [128, 128], BF16)
    cmP, cmPB = K.sb("cmP", [128, 5, 128], F32)
    cmS, cmSB = K.sb("cmS", [128, 5, 128], F32)
    ones_f, onesfB = K.sb("ones_f", [128, 128], F32)
    bm, bmB = K.sb("bm", [128, 16, 128], BF16)
    rm, rmB = K.sb("rm", [128, 16], F32)
    wba, wbaB = K.sb("wba", [128, KC, 2 * NH], BF16)
    wpl, wplB = K.sb("wpl", [128, 4, 2, 256], BF16)
    vps, vpsB = K.sb("vps", [128, PWC], F32)
    vwc, vwcB = K.sb("vwc", [128, 3 * NH, 4], F32)
    vhb, vhbB = K.sb("vhb", [128, 2 * NH], F32)
    von, vonB = K.sb("von", [128, 1], F32)
    nea, neaB = K.sb("nea", [128, NH], F32)
    inv_c, invcB = K.sb("inv_c", [128, 4, 16], F32)
    sdr = nc.dram_tensor("sdr", [NH, 128, 128], F32, kind="Internal").ap()
    sdrB = [K.buf() for _ in range(NH)]

    K.dma(sp, ident_f[:], c_id, writes=[identfB])
    K.dma(pool, ident_b[:], c_id, writes=[identbB])
    K.dma(sp, cmP[:], cm_p, writes=[cmPB])
    K.dma(sp, cmS[:], cm_s, writes=[cmSB])
    K.dma(pool, bm[:], c_bm, writes=[bmB])
    K.dma(sp, rm[:], c_rm, writes=[rmB])
    K.dma(pool, wba[:].rearrange("p a b -> p (a b)"), w_ba, writes=[wbaB])
    K.dma(pool, wpl[:].rearrange("p a b c -> p (a b c)"), w_pl, writes=[wplB])
    K.dma(sp, vps[:], v_ps, writes=[vpsB])
    K.dma(sp, vwc[:].rearrange("p a b -> p (a b)"), v_wc, writes=[vwcB])
    K.dma(sp, vhb[:], v_hb, writes=[vhbB])
    K.dma(sp, von[:], v_on, writes=[vonB])
    K.dma(sp, inv_c[:], invc, writes=[invcB])
    K.op(dve, lambda: V.memset(ones_b[:], 1.0), writes=[onesB])
    K.op(dve, lambda: V.memset(ones_f[:], 1.0), writes=[onesfB])
    K.op(act, lambda: A.activation(out=nea[:], in_=vhb[:, NH:2 * NH], func=AF.Exp), reads=[vhbB], writes=[neaB])
    K.op(dve, lambda: V.tensor_scalar(out=nea[:], in0=nea[:], scalar1=-1.0, scalar2=None, op0=ALU.mult),
         reads=[neaB], writes=[neaB])
    K.op(dve, lambda: V.tensor_scalar(out=von[:], in0=von[:], scalar1=math.sqrt(128.0), scalar2=None, op0=ALU.mult),
         reads=[vonB], writes=[vonB])

    banks = []
    for i in range(7):
        t = K.es.enter_context(nc.psum_tensor(f"pb{i}", [128, 512], F32))
        banks.append((t, K.buf(f"pb{i}")))
    pbf, pbfB = K.es.enter_context(nc.psum_tensor("pbf", [128, 1024], BF16)), K.buf("pbf")

    def bank():
        K.rr = (K.rr + 1) % 7
        return banks[K.rr]

    tog = [0]

    def evac(out_ap, in_ap, reads, writes, eng=None):
        if eng is None:
            tog[0] ^= 1
            eng = act if tog[0] else dve
        if eng is act:
            K.op(act, lambda: A.copy(out=out_ap, in_=in_ap), reads=reads, writes=writes)
        else:
            K.op(dve, lambda: V.tensor_copy(out=out_ap, in_=in_ap), reads=reads, writes=writes)

    def rsqrt(out_ap, in_ap, eps, reads, writes):
        K.op(act, lambda: A.activation(out=out_ap, in_=in_ap, func=AF.Ln, bias=eps), reads=reads, writes=writes)
        K.op(act, lambda: A.activation(out=out_ap, in_=out_ap, func=AF.Exp, scale=-0.5), reads=writes, writes=writes)

    def build_T(es, dstT, dstB, col0, src_rows_ap, nrows, xr, xrB, q=None):
        K.dma(q or sp, xr[0:nrows, :], src_rows_ap, writes=[xrB])
        transpose_rows(dstT, dstB, col0, xr, xrB, nrows)

    def transpose_rows(dstT, dstB, col0, xr, xrB, nrows):
        for k0 in range(0, KC, 4):
            kn = min(4, KC - k0)
            pb, pbB = bank()
            for j in range(kn):
                K.mm(lambda j=j: T.transpose(out=pb[:, j * 128:j * 128 + nrows],
                                             in_=xr[0:nrows, (k0 + j) * 128:(k0 + j + 1) * 128],
                                             identity=ident_f[0:nrows, 0:nrows]),
                     reads=[xrB, identfB], writes=[pbB], inc=(j == kn - 1))
            evac(dstT[:, k0:k0 + kn, col0:col0 + nrows],
                 pb[:, 0:kn * 128].rearrange("p (a b) -> p a b", b=128)[:, :, 0:nrows],
                 [pbB], [dstB])

    def tile_pre(es, xT, xTB, col0, cm, cmB, TP, TPB, t):
        tmp, tmpB = tp_tmp
        pb, pbB = bank()
        for kc in range(KC):
            K.mm(lambda kc=kc: T.matmul(pb[:, 0:2 * NH], lhsT=xT[:, kc, col0:col0 + 128], rhs=wba[:, kc, :],
                                        start=(kc == 0), stop=(kc == KC - 1)),
                 reads=[xTB, wbaB], writes=[pbB], inc=(kc == KC - 1))
        lb = TP[:, t, 6, :]
        g = TP[:, t, 5, :]
        sc = tmp[:, 2, :]
        K.op(act, lambda: A.activation(out=sc, in_=pb[:, 0:NH], func=AF.Exp, scale=-1.0), reads=[pbB], writes=[tmpB])
        K.op(act, lambda: A.activation(out=sc, in_=sc, func=AF.Ln, bias=1.0), reads=[tmpB], writes=[tmpB])
        K.op(dve, lambda: V.tensor_scalar(out=lb, in0=sc, scalar1=-1.0, scalar2=None, op0=ALU.mult), reads=[tmpB], writes=[TPB])
        K.op(act, lambda: A.activation(out=TP[:, t, 0, :], in_=lb, func=AF.Exp), reads=[TPB], writes=[TPB])
        K.op(dve, lambda: V.tensor_tensor(out=sc, in0=pb[:, NH:2 * NH], in1=vhb[:, 0:NH], op=ALU.add), reads=[pbB, vhbB], writes=[tmpB])
        K.op(act, lambda: A.activation(out=sc, in_=sc, func=AF.Exp), reads=[tmpB], writes=[tmpB])
        K.op(act, lambda: A.activation(out=sc, in_=sc, func=AF.Ln, bias=1.0), reads=[tmpB], writes=[tmpB])
        K.op(dve, lambda: V.tensor_tensor(out=g, in0=sc, in1=nea[:], op=ALU.mult), reads=[tmpB, neaB], writes=[TPB])
        assert NH % 8 == 0
        for hc in range(NH // 8):
            hs = slice(hc * 8, hc * 8 + 8)
            pb2, pb2B = bank()
            gh = TP[:, t, 5, hs]
            for j in range(3):
                K.mm(lambda j=j: T.matmul(pb2[:, j * 8:(j + 1) * 8], lhsT=cm[:, j, :], rhs=gh, start=True, stop=True),
                     reads=[cmB, TPB], writes=[pb2B], inc=(j == 2))
            sch = tmp[:, 2, hs]
            K.op(dve, lambda: V.tensor_scalar(out=TP[:, t, 1, hs], in0=pb2[:, 0:8], scalar1=-1.0, scalar2=None, op0=ALU.mult),
                 reads=[pb2B], writes=[TPB])
            K.op(act, lambda: A.activation(out=sch, in_=pb2[:, 0:8], func=AF.Exp), reads=[pb2B], writes=[tmpB])
            K.op(dve, lambda: V.tensor_tensor(out=TP[:, t, 2, hs], in0=sch, in1=TP[:, t, 0, hs], op=ALU.mult), reads=[tmpB, TPB], writes=[TPB])
            K.op(act, lambda: A.activation(out=TP[:, t, 3, hs], in_=pb2[:, 8:16], func=AF.Exp), reads=[pb2B], writes=[TPB])
            K.op(act, lambda: A.activation(out=TP[:, t, 4, hs], in_=pb2[:, 16:24], func=AF.Exp), reads=[pb2B], writes=[TPB])

    def unit(u, h, t, tcols, TP, TPB, cm, cmB, Kb, Qb, Kn, Vf, OT, hdB, full, sample, nlev):
        def col(j):
            return TP[:, t, j, h:h + 1]
        A_, A_B = bank()
        K.mm(lambda: T.matmul(A_[:, 0:128], lhsT=Kb[:, tcols], rhs=Kb[:, tcols], start=True, stop=True),
             reads=[hdB], writes=[A_B], inc=not full)
        if full:
            K.mm(lambda: T.matmul(A_[:, 128:256], lhsT=Kb[:, tcols], rhs=Qb[:, tcols], start=True, stop=True),
                 reads=[hdB], writes=[A_B])
        B_, B_B = bank()
        nb_ = 384 if sample else 256
        GR = u["GR"]
        K.op(dve, lambda: V.tensor_scalar(out=GR[0][:, 0:128], in0=cm[:, 0, :], scalar1=col(5), scalar2=None, op0=ALU.mult),
             reads=[cmB, TPB], writes=[GR[1]])
        K.op(dve, lambda: V.scalar_tensor_tensor(out=GR[0][:, 128:256], in0=ident_f[:], scalar=col(6), in1=GR[0][:, 0:128], op0=ALU.mult, op1=ALU.add),
             reads=[identfB, TPB, GR[1]], writes=[GR[1]])
        if sample:
            K.op(dve, lambda: V.tensor_scalar(out=GR[0][:, 256:384], in0=cm[:, 2, :], scalar1=col(5), scalar2=None, op0=ALU.mult),
                 reads=[cmB, TPB], writes=[GR[1]])
        for q_ in range(nb_ // 128):
            K.mm(lambda q_=q_: T.matmul(B_[:, q_ * 128:(q_ + 1) * 128], lhsT=ones_f[:], rhs=GR[0][:, q_ * 128:(q_ + 1) * 128], start=True, stop=True),
                 reads=[onesfB, GR[1]], writes=[B_B], inc=(q_ == nb_ // 128 - 1))
        T2, D2 = u["T2"], u["D2"]
        K.op(dve, lambda: V.tensor_tensor(out=T2[0][:], in0=B_[:, 0:256], in1=cm[:, 3:5, :].rearrange("p a b -> p (a b)"), op=ALU.add),
             reads=[B_B, cmB], writes=[T2[1]])
        K.op(act, lambda: A.activation(out=D2[0][:], in_=T2[0][:], func=AF.Exp, bias=col(1)), reads=[T2[1], TPB], writes=[D2[1]])
        if full:
            K.op(dve, lambda: V.tensor_tensor(out=u["QK"][0][:], in0=A_[:, 128:256], in1=D2[0][:, 0:128], op=ALU.mult),
                 reads=[A_B, D2[1]], writes=[u["QK"][1]])
            E = u["E"]
            K.op(act, lambda: A.activation(out=E[0][:], in_=B_[:, 0:128], func=AF.Exp), reads=[B_B], writes=[E[1]])
            K.op(dve, lambda: V.tensor_tensor(out=u["QS"][0][:], in0=Qb[:, tcols], in1=E[0][:], op=ALU.mult),
                 reads=[hdB, E[1]], writes=[u["QS"][1]])
        if sample:
            Et = u["Etot"]
            K.op(act, lambda: A.activation(out=Et[0][:], in_=B_[:, 256:384], func=AF.Exp), reads=[B_B], writes=[Et[1]])
        NTl, Nl, RT = u["NT"], u["N"], u["RT"]
        K.op(dve, lambda: V.scalar_tensor_tensor(out=NTl[0][0][:], in0=A_[:, 0:128], scalar=-1.0, in1=D2[0][:, 128:256], op0=ALU.mult, op1=ALU.mult),
             reads=[A_B, D2[1]], writes=[NTl[0][1]])
        K.op(dve, lambda: V.tensor_tensor(out=RT[0][:], in0=NTl[0][0][:], in1=ident_f[:], op=ALU.add),
             reads=[NTl[0][1], identfB], writes=[RT[1]])
        p0, p0B = bank()
        K.mm(lambda: T.transpose(out=p0[:, 0:128], in_=NTl[0][0][:], identity=ident_f[:]), reads=[NTl[0][1], identfB], writes=[p0B])
        evac(Nl[0][0][:], p0[:, 0:128], [p0B], [Nl[0][1]])
        for l in range(1, nlev + 1):
            p1, p1B = bank()
            K.mm(lambda l=l: T.matmul(p1[:, 0:128], lhsT=NTl[l - 1][0][:], rhs=Nl[l - 1][0][:], start=True, stop=True),
                 reads=[NTl[l - 1][1], Nl[l - 1][1]], writes=[p1B])
            evac(Nl[l][0][:], p1[:, 0:128], [p1B], [Nl[l][1]], eng=act)
            if l < nlev:
                K.mm(lambda l=l: T.matmul(p1[:, 128:256], lhsT=Nl[l - 1][0][:], rhs=NTl[l - 1][0][:], start=True, stop=True),
                     reads=[NTl[l - 1][1], Nl[l - 1][1]], writes=[p1B])
                evac(NTl[l][0][:], p1[:, 128:256], [p1B], [NTl[l][1]], eng=dve)
            p3, p3B = bank()
            K.mm(lambda l=l: T.matmul(p3[:, 0:128], lhsT=Nl[l][0][:], rhs=RT[0][:], start=True, stop=True),
                 reads=[Nl[l][1], RT[1]], writes=[p3B])
            K.op(dve, lambda: V.tensor_tensor(out=RT[0][:], in0=RT[0][:], in1=p3[:, 0:128], op=ALU.add),
                 reads=[RT[1], p3B], writes=[RT[1]])
        RTf = RT
        RT = u["RTb"]
        K.op(act, lambda: A.copy(out=RT[0][:], in_=RTf[0][:]), reads=[RTf[1]], writes=[RT[1]])
        PT, PTB = bank()
        K.mm(lambda: T.transpose(out=PT[:, 0:128], in_=Kn[:, tcols], identity=ident_f[:]), reads=[hdB, identfB], writes=[PTB], inc=False)
        K.mm(lambda: T.transpose(out=PT[:, 128:256], in_=Vf[:, tcols], identity=ident_f[:]), reads=[hdB, identfB], writes=[PTB])
        KBG, KD, VB, NWT, VN = u["KBG"], u["KD"], u["VB"], u["NWT"], u["VN"]
        K.op(dve, lambda: V.tensor_scalar(out=KBG[0][:], in0=PT[:, 0:128], scalar1=col(2), scalar2=None, op0=ALU.mult),
             reads=[PTB, TPB], writes=[KBG[1]])
        K.op(dve, lambda: V.tensor_scalar(out=KD[0][:], in0=PT[:, 0:128], scalar1=col(3), scalar2=None, op0=ALU.mult),
             reads=[PTB, TPB], writes=[KD[1]])
        K.op(dve, lambda: V.tensor_scalar(out=VB[0][:], in0=PT[:, 128:256], scalar1=col(0), scalar2=None, op0=ALU.mult),
             reads=[PTB, TPB], writes=[VB[1]])
        PW_, PWB = bank()
        K.mm(lambda: T.matmul(PW_[:, 0:128], lhsT=KBG[0][:], rhs=RT[0][:], start=True, stop=True), reads=[KBG[1], RT[1]], writes=[PWB])
        K.op(act, lambda: A.mul(out=NWT[0][:], in_=PW_[:, 0:128], mul=-1.0), reads=[PWB], writes=[NWT[1]])
        Sh = u["S"][0][:]
        ShB = u["S"][1]
        SBh = u["SBh"]
        if not sample:
            PV, PVB = bank()
            K.mm(lambda: T.matmul(PV[:, 0:128], lhsT=RT[0][:], rhs=VB[0][:], start=True, stop=False), reads=[RT[1], VB[1]], writes=[PVB], inc=False)
            K.mm(lambda: T.matmul(PV[:, 0:128], lhsT=NWT[0][:], rhs=SBh[0][:], start=False, stop=True), reads=[NWT[1], SBh[1]], writes=[PVB])
            evac(VN[0][:], PV[:, 0:128], [PVB], [VN[1]], eng=dve)
            if full:
                PO, POB = bank()
                K.mm(lambda: T.matmul(PO[:, 0:128], lhsT=SBh[0][:], rhs=u["QS"][0][:], start=True, stop=False), reads=[SBh[1], u["QS"][1]], writes=[POB], inc=False)
                K.mm(lambda: T.matmul(PO[:, 0:128], lhsT=VN[0][:], rhs=u["QK"][0][:], start=False, stop=True), reads=[VN[1], u["QK"][1]], writes=[POB])
                evac(OT[0][:, tcols], PO[:, 0:128], [POB], [OT[1]], eng=act)
            PS_, PSB = bank()
            K.mm(lambda: T.matmul(PS_[:, 0:128], lhsT=KD[0][:], rhs=VN[0][:], start=True, stop=True), reads=[KD[1], VN[1]], writes=[PSB])
            K.op(dve, lambda: V.scalar_tensor_tensor(out=Sh, in0=Sh, scalar=col(4), in1=PS_[:, 0:128], op0=ALU.mult, op1=ALU.add),
                 reads=[ShB, TPB, PSB], writes=[ShB])
            K.op(act, lambda: A.copy(out=SBh[0][:], in_=Sh), reads=[ShB], writes=[SBh[1]])
        else:
            Ss, SBs, M1 = u["Ss"], u["SBs"], u["M1"]
            M2 = M1
            K.op(dve, lambda: V.tensor_tensor(out=M1[0][:], in0=NWT[0][:].unsqueeze(1).broadcast_to([128, 16, 128]), in1=bm[:], op=ALU.mult),
                 reads=[NWT[1], bmB], writes=[M1[1]])
            PV, PVB = bank()
            K.mm(lambda: T.matmul(PV[:, 0:128], lhsT=RT[0][:], rhs=VB[0][:], start=True, stop=False), reads=[RT[1], VB[1]], writes=[PVB], inc=False)
            for b in range(16):
                K.mm(lambda b=b: T.matmul(PV[:, 0:128], lhsT=M1[0][:, b, :], rhs=SBs[0][:, b, :], start=False, stop=(b == 15)),
                     reads=[M1[1], SBs[1]], writes=[PVB], inc=(b == 15))
            evac(VN[0][:], PV[:, 0:128], [PVB], [VN[1]], eng=dve)
            K.op(dve, lambda: V.tensor_tensor(out=M2[0][:], in0=u["QS"][0][:].unsqueeze(1).broadcast_to([128, 16, 128]), in1=bm[:], op=ALU.mult),
                 reads=[u["QS"][1], bmB], writes=[M2[1]])
            PO, POB = bank()
            for b in range(16):
                K.mm(lambda b=b: T.matmul(PO[:, 0:128], lhsT=SBs[0][:, b, :], rhs=M2[0][:, b, :], start=(b == 0), stop=False),
                     reads=[M2[1], SBs[1]], writes=[POB], inc=False)
            K.mm(lambda: T.matmul(PO[:, 0:128], lhsT=VN[0][:], rhs=u["QK"][0][:], start=False, stop=True), reads=[VN[1], u["QK"][1]], writes=[POB])
            evac(OT[0][:, tcols], PO[:, 0:128], [POB], [OT[1]], eng=act)
            K.op(dve, lambda: V.tensor_tensor(out=M1[0][:], in0=KD[0][:].unsqueeze(1).broadcast_to([128, 16, 128]),
                                              in1=rm[:].unsqueeze(2).broadcast_to([128, 16, 128]), op=ALU.mult),
                 reads=[KD[1], rmB], writes=[M1[1]])
            Et = u["Etot"]
            for r in range(4):
                PX, PXB = bank()
                for bb in range(4):
                    b = 4 * r + bb
                    K.mm(lambda b=b, bb=bb: T.matmul(PX[:, bb * 128:(bb + 1) * 128], lhsT=M1[0][:, b, :], rhs=VN[0][:], start=True, stop=True),
                         reads=[M1[1], VN[1]], writes=[PXB], inc=(bb == 3))
                sv = Ss[0][:, 4 * r:4 * r + 4, :]
                K.op(dve, lambda r=r, sv=sv: V.tensor_tensor(out=sv, in0=sv,
                                                            in1=Et[0][:, 32 * r:32 * r + 32].rearrange("p (a b) -> p a b", b=8)[:, :, 0:1].broadcast_to([128, 4, 128]),
                                                            op=ALU.mult),
                     reads=[Ss[1], Et[1]], writes=[Ss[1]])
                K.op(dve, lambda sv=sv, PX=PX: V.tensor_tensor(out=sv, in0=sv, in1=PX[:, 0:512].rearrange("p (a b) -> p a b", b=128), op=ALU.add),
                     reads=[Ss[1], PXB], writes=[Ss[1]])

    def mixer_pass(full):
        es = ExitStack()
        ncol = MW if full else HALF
        xT, xTB = K.sb("xT", [128, KC, ncol], BF16, es)
        ntile = NT + 1 if full else NT
        TP, TPB = K.sb("TP", [128, ntile, 7, NH], F32, es)
        slabs = [K.sb(f"mslab{i}", [128, KC * 128], BF16, es) for i in range(2)]
        global tp_tmp
        tp_tmp = K.sb("tp_tmp", [128, 3, NH], F32, es)
        with ExitStack() as es2:
            xr = [K.sb(f"xr{i}", [128, D], F32, es2) for i in range(2)]
            n = 0
            if full:
                build_T(es2, xT, xTB, 0, xs[HALF - 16:HALF, :], 16, xr[0][0], xr[0][1])
                n = 1
                for t in range(NT + 1):
                    r0 = HALF + t * 128
                    build_T(es2, xT, xTB, 16 + t * 128, xs[r0:r0 + 128, :], 128, xr[n % 2][0], xr[n % 2][1])
                    n += 1
            else:
                for t in range(NT):
                    build_T(es2, xT, xTB, t * 128, xs[t * 128:(t + 1) * 128, :], 128, xr[n % 2][0], xr[n % 2][1])
                    n += 1
        K.barrier()
        if stage == 1.1:
            es.close()
            return
        base = 16 if full else 0
        for t in range(ntile):
            samp = full and t == NT
            tile_pre(es, xT, xTB, base + t * 128, cmS if samp else cmP, cmSB if samp else cmPB, TP, TPB, t)
        if stage == 1.2:
            es.close()
            return
        if full:
            pool_mixer(xT, xTB, slabs)
            K.barrier()
        with ExitStack() as es3:
            nx = 4 if full else 2
            LP = 16 + HALF if full else 3 + HALF
            PRE = [K.sb(f"PRE{i}", [128, LP], F32, es3) for i in range(3)]
            NTOK = NMAIN if full else HALF
            X = [K.sb(f"X{i}", [128, NTOK], F32, es3) for i in range(3)]
            SQ, SQB = K.sb("SQ", [128, NTOK], BF16, es3)
            RN, RNB = K.sb("RN", [128, 512], F32, es3)
            Kb, _ = K.sb("Kb", [128, NTOK], BF16, es3)
            hdB = K.buf("hd")
            u = {}
            for nm, shp, dt in [("GR", [128, 384], F32), ("S", [128, 128], F32), ("T2", [128, 256], F32), ("D2", [128, 256], F32), ("QK", [128, 128], BF16), ("E", [128, 128], F32),
                                ("QS", [128, 128], BF16), ("RT", [128, 128], F32), ("RTb", [128, 128], BF16), ("KBG", [128, 128], BF16), ("KD", [128, 128], BF16),
                                ("VB", [128, 128], BF16), ("NWT", [128, 128], BF16), ("VN", [128, 128], BF16), ("SBh", [128, 128], BF16)]:
                u[nm] = K.sb("u_" + nm, shp, dt, es3)
            u["NT"] = [K.sb(f"u_NT{i}", [128, 128], F32, es3) for i in range(7)]
            u["N"] = [K.sb(f"u_N{i}", [128, 128], F32, es3) for i in range(7)]
            if full:
                PREs = [K.sb(f"PREs{i}", [128, 16, 11], F32, es3) for i in range(3)]
                Qb, _ = K.sb("Qb", [128, NTOK], BF16, es3)
                SZ, SZB = K.sb("SZ", [128, NTOK], BF16, es3)
                OT = X[2]
                MXs = K.sb("MXs", [128, NTOK], BF16, es3)
                sch = K.sb("sch", [48, 3, 128], F32, es3)
                nco = K.sb("nco", [48, 384], F32, es3)
                ncp = K.sb("ncp", [3, 384], F32, es3)
                t48 = K.sb("t48", [128, 3, 48], F32, es3)
                u["Etot"] = K.sb("u_Etot", [128, 128], F32, es3)
                u["Ss"] = K.sb("u_Ss", [128, 16, 128], F32, es3)
                u["SBs"] = K.sb("u_SBs", [128, 16, 128], BF16, es3)
                u["M1"] = K.sb("u_M1", [128, 16, 128], BF16, es3)
            else:
                Qb = None
                OT = None
                for i in range(2):
                    K.op(dve, lambda i=i: V.memset(PRE[i][0][:, 0:3], 0.0), writes=[PRE[i][1]])
            srcs = []
            for h in range(NH):
                for si in range(4 if full else 2):
                    srcs.append((w_hd[h][si], KC * 128))
            stream = SlabStream(K, slabs, srcs)
            pgroups = split_groups(0, 16 + HALF) if full else split_groups(0, HALF)
            for h in range(NH):
                if full:
                    K.dma(sp, u["S"][0][:], sdr[h], reads=[sdrB[h]], writes=[u["S"][1]])
                else:
                    K.op(dve, lambda: V.memset(u["S"][0][:], 0.0), writes=[u["S"][1]])
                for si in range(4 if full else 2):
                    slab, slabB = stream.next()
                    sv = slab[:, 0:KC * 128].rearrange("p (k n) -> p k n", n=128)
                    for j in range(1):
                        xi = si
                        for (c0, n) in pgroups:
                            pb, pbB = bank()
                            for kc in range(KC):
                                K.mm(lambda kc=kc: T.matmul(pb[:, 0:n], lhsT=sv[:, kc, 0:128], rhs=xT[:, kc, c0:c0 + n],
                                                            start=(kc == 0), stop=(kc == KC - 1)),
                                     reads=[slabB, xTB], writes=[pbB], inc=(kc == KC - 1))
                            if xi < 3:
                                off = 0 if full else 3
                                evac(PRE[xi][0][:, off + c0:off + c0 + n], pb[:, 0:n], [pbB], [PRE[xi][1]])
                            else:
                                m0 = max(c0, 16)
                                K.op(act, lambda: A.activation(out=SZ[:, m0 - 16:c0 + n - 16], in_=pb[:, m0 - c0:n], func=AF.Silu),
                                     reads=[pbB], writes=[SZB])
                        if full:
                            pb, pbB = bank()
                            c0 = 16 + HALF
                            for kc in range(KC):
                                K.mm(lambda kc=kc: T.matmul(pb[:, 0:128], lhsT=sv[:, kc, 0:128], rhs=xT[:, kc, c0:c0 + 128],
                                                            start=(kc == 0), stop=(kc == KC - 1)),
                                     reads=[slabB, xTB], writes=[pbB], inc=(kc == KC - 1))
                            if xi < 3:
                                evac(PREs[xi][0][:, :, 3:11], pb[:, 0:128].rearrange("p (a b) -> p a b", b=8), [pbB], [PREs[xi][1]])
                            else:
                                K.op(act, lambda: A.activation(out=SZ[:, HALF:HALF + 128], in_=pb[:, 0:128], func=AF.Silu),
                                     reads=[pbB], writes=[SZB])
                if full:
                    K.dma(sp, sch[0][:], sconv[:, :, h, :], writes=[sch[1]])
                    K.dma(sp, u["Ss"][0][:], sdelta[:, h].rearrange("b k v -> k b v"), writes=[u["Ss"][1]])
                    K.op(act, lambda: A.copy(out=u["SBs"][0][:], in_=u["Ss"][0][:]), reads=[u["Ss"][1]], writes=[u["SBs"][1]])
                    pb, pbB = bank()
                    for x3 in range(3):
                        K.mm(lambda x3=x3: T.transpose(out=pb[:, x3 * 64:x3 * 64 + 48], in_=sch[0][:, x3, :], identity=ident_f[0:48, 0:48]),
                             reads=[sch[1], identfB], writes=[pbB], inc=(x3 == 2))
                    for x3 in range(3):
                        xi = (x3 + 2) % 3
                        evac(PREs[xi][0][:, :, 0:3], pb[:, x3 * 64:x3 * 64 + 48].rearrange("p (a b) -> p a b", b=3), [pbB], [PREs[xi][1]])
                    pb, pbB = bank()
                    pb2, pb2B = bank()
                    for x3 in range(3):
                        xi = (x3 + 2) % 3
                        K.op(dve, lambda x3=x3, xi=xi: V.tensor_copy(out=t48[0][:, x3, :].rearrange("p (a b) -> p a b", b=3), in_=PREs[xi][0][:, :, 8:11]),
                             reads=[PREs[xi][1]], writes=[t48[1]])
                        K.mm(lambda x3=x3, xi=xi: T.transpose(out=pb[0:48, x3 * 128:(x3 + 1) * 128], in_=t48[0][:, x3, :], identity=ident_f[:]),
                             reads=[t48[1], identfB], writes=[pbB], inc=(x3 == 2))
                    for x3 in range(3):
                        xi = (x3 + 2) % 3
                        K.mm(lambda x3=x3, xi=xi: T.transpose(out=pb2[0:3, x3 * 128:(x3 + 1) * 128], in_=PRE[xi][0][:, 16 + HALF - 3:16 + HALF], identity=ident_f[:]),
                             reads=[PRE[xi][1], identfB], writes=[pb2B], inc=(x3 == 2))
                    evac(nco[0][:], pb[0:48, 0:384], [pbB], [nco[1]])
                    evac(ncp[0][:], pb2[0:3, 0:384], [pb2B], [ncp[1]])
                    K.dma(sp, o_ncs[:, :, h, :], nco[0][:].rearrange("p (a b) -> p a b", b=128), reads=[nco[1]], is_output=True)
                    K.dma(sp, o_ncp[:, :, h, :], ncp[0][:].rearrange("p (a b) -> p a b", b=128), reads=[ncp[1]], is_output=True)
                cb0 = 13 if full else 0
                for xi in range(3 if full else 2):
                    ci = ((xi + 1) % 3) * NH + h
                    acc = X[xi][0]
                    K.op(dve, lambda: V.tensor_scalar(out=acc[:, 0:HALF], in0=PRE[xi][0][:, cb0:cb0 + HALF], scalar1=vwc[:, ci, 0:1], scalar2=None, op0=ALU.mult),
                         reads=[PRE[xi][1], vwcB, hdB], writes=[X[xi][1], hdB])
                    for j in range(1, 4):
                        K.op(dve, lambda j=j: V.scalar_tensor_tensor(out=acc[:, 0:HALF], in0=PRE[xi][0][:, cb0 + j:cb0 + j + HALF], scalar=vwc[:, ci, j:j + 1],
                                                                      in1=acc[:, 0:HALF], op0=ALU.mult, op1=ALU.add),
                             reads=[PRE[xi][1], vwcB], writes=[X[xi][1]])
                    if full:
                        accs = acc[:, HALF:HALF + 128].rearrange("p (a b) -> p a b", b=8)
                        K.op(dve, lambda: V.tensor_scalar(out=accs, in0=PREs[xi][0][:, :, 0:8], scalar1=vwc[:, ci, 0:1], scalar2=None, op0=ALU.mult),
                             reads=[PREs[xi][1], vwcB], writes=[X[xi][1]])
                        for j in range(1, 4):
                            K.op(dve, lambda j=j: V.scalar_tensor_tensor(out=accs, in0=PREs[xi][0][:, :, j:j + 8], scalar=vwc[:, ci, j:j + 1],
                                                                          in1=accs, op0=ALU.mult, op1=ALU.add),
                                 reads=[PREs[xi][1], vwcB], writes=[X[xi][1]])
                    K.op(act, lambda: A.activation(out=acc[:, 0:NTOK], in_=acc[:, 0:NTOK], func=AF.Silu), reads=[X[xi][1]], writes=[X[xi][1]])
                for xi in ([0, 2] if full else [0]):
                    acc = X[xi][0]
                    K.op(act, lambda: A.activation(out=SQ[:], in_=acc[:, 0:NTOK], func=AF.Square), reads=[X[xi][1]], writes=[SQB])
                    for (c0, n) in split_groups(0, NTOK):
                        pb, pbB = bank()
                        K.mm(lambda: T.matmul(pb[:, 0:n], lhsT=ones_b[:], rhs=SQ[:, c0:c0 + n], start=True, stop=True), reads=[onesB, SQB], writes=[pbB])
                        rsqrt(RN[:, 0:n], pb[:, 0:n], 1e-6, [pbB], [RNB])
                        if xi == 0:
                            K.op(dve, lambda: V.tensor_tensor(out=acc[:, c0:c0 + n], in0=acc[:, c0:c0 + n], in1=RN[:, 0:n], op=ALU.mult),
                                 reads=[X[xi][1], RNB], writes=[X[xi][1]])
                        else:
                            K.op(dve, lambda: V.scalar_tensor_tensor(out=Qb[:, c0:c0 + n], in0=acc[:, c0:c0 + n], scalar=128.0 ** -0.5, in1=RN[:, 0:n],
                                                                      op0=ALU.mult, op1=ALU.mult),
                                 reads=[X[xi][1], RNB], writes=[hdB])
                K.op(act, lambda: A.copy(out=Kb[:], in_=X[0][0][:, 0:NTOK]), reads=[X[0][1]], writes=[hdB])
                K.op(dve, lambda: V.tensor_copy(out=u["SBh"][0][:], in_=u["S"][0][:]), reads=[u["S"][1], X[0][1], X[1][1]], writes=[u["SBh"][1], hdB])
                for t in range(ntile if stage != 1.3 else 0):
                    samp = full and t == NT
                    tcols = slice(t * 128, (t + 1) * 128)
                    unit(u, h, t, tcols, TP, TPB, cmS if samp else cmP, cmSB if samp else cmPB,
                         Kb, Qb, X[0][0], X[1][0], OT, hdB, full, samp, 2 if samp else 6)
                if not full:
                    K.dma(sp, sdr[h], u["S"][0][:], reads=[u["S"][1]], writes=[sdrB[h]])
                if full:
                    K.dma(sp, o_ndp[h], u["S"][0][:], reads=[u["S"][1]], is_output=True)
                    K.dma(sp, o_nds[:, h].rearrange("b k v -> k b v"), u["Ss"][0][:], reads=[u["Ss"][1]], is_output=True)
                    K.op(act, lambda: A.activation(out=SQ[:], in_=OT[0][:], func=AF.Square), reads=[OT[1]], writes=[SQB])
                    for (c0, n) in split_groups(0, NTOK):
                        pb, pbB = bank()
                        K.mm(lambda: T.matmul(pb[:, 0:n], lhsT=ones_b[:], rhs=SQ[:, c0:c0 + n], start=True, stop=True), reads=[onesB, SQB], writes=[pbB])
                        rsqrt(RN[:, 0:n], pb[:, 0:n], 128.0 * 1e-6, [pbB], [RNB])
                        K.op(dve, lambda: V.tensor_tensor(out=OT[0][:, c0:c0 + n], in0=OT[0][:, c0:c0 + n], in1=RN[:, 0:n], op=ALU.mult),
                             reads=[OT[1], RNB], writes=[OT[1]])
                    K.op(dve, lambda: V.scalar_tensor_tensor(out=MXs[0][:], in0=OT[0][:], scalar=von[:, 0:1], in1=SZ[:], op0=ALU.mult, op1=ALU.mult),
                         reads=[OT[1], vonB, SZB], writes=[MXs[1]])
                    K.dma(sp, mixd[PWC + h], MXs[0][:], reads=[MXs[1]], writes=[mixB[PWC + h]])
        K.barrier()
        es.close()
        K.barrier()

    def pool_mixer(xT, xTB, slabs):
        with ExitStack() as es:
            LU = 16 + HALF
            U = [K.sb(f"pU{i}", [128, LU], F32, es) for i in range(2)]
            EX = [K.sb(f"pE{i}", [128, 16, 23], F32, es) for i in range(2)]
            W1, W1B = K.sb("pW1", [128, LU], F32, es)
            W2, W2B = K.sb("pW2", [128, LU], F32, es)
            V1, V1B = K.sb("pV1", [128, 16, 23], F32, es)
            V2, V2B = K.sb("pV2", [128, 16, 23], F32, es)
            Dd = [K.sb(f"pD{i}", [128, NMAIN], BF16, es) for i in range(2)]
            spl = [K.sb(f"pS{i}", [120, 128], F32, es) for i in range(2)]
            npp = K.sb("pnpp", [15, 128], F32, es)
            nps = K.sb("pnps", [128, 128], F32, es)
            mxs = K.sb("pmx", [128, NMAIN], BF16, es)
            usc = K.sb("pusc", [128, 128], F32, es)
            stream = SlabStream(K, slabs, [(w_u[i], KC * 128) for i in range(8)])
            pgroups = split_groups(0, LU)
            for gi in range(4):
                w = 2 ** (gi + 1)
                for j in range(2):
                    slab, slabB = stream.next()
                    sv = slab[:, 0:KC * 128].rearrange("p (k n) -> p k n", n=128)
                    c = 2 * gi + j
                    Uc, UcB = U[j]
                    Ec, EcB = EX[j]
                    for (c0, n) in pgroups:
                        pb, pbB = bank()
                        for kc in range(KC):
                            K.mm(lambda kc=kc: T.matmul(pb[:, 0:n], lhsT=sv[:, kc, 0:128], rhs=xT[:, kc, c0:c0 + n],
                                                        start=(kc == 0), stop=(kc == KC - 1)),
                                 reads=[slabB, xTB], writes=[pbB], inc=(kc == KC - 1))
                        evac(Uc[:, c0:c0 + n], pb[:, 0:n], [pbB], [UcB])
                    pb, pbB = bank()
                    c0 = LU
                    for kc in range(KC):
                        K.mm(lambda kc=kc: T.matmul(pb[:, 0:128], lhsT=sv[:, kc, 0:128], rhs=xT[:, kc, c0:c0 + 128],
                                                    start=(kc == 0), stop=(kc == KC - 1)),
                             reads=[slabB, xTB], writes=[pbB], inc=(kc == KC - 1))
                    evac(Ec[:, :, 15:23], pb[:, 0:128].rearrange("p (a b) -> p a b", b=8), [pbB], [EcB])
                    for i in range(2):
                        K.dma(sp, spl[i][0][:], spool[8 * i:8 * i + 8, :, c * 128:(c + 1) * 128].rearrange("b j c -> (b j) c"), writes=[spl[i][1]])
                    pb, pbB = bank()
                    for i in range(2):
                        K.mm(lambda i=i: T.transpose(out=pb[:, i * 128:i * 128 + 120], in_=spl[i][0][:, 0:128], identity=ident_f[0:120, 0:120]),
                             reads=[spl[i][1], identfB], writes=[pbB], inc=(i == 1))
                    for i in range(2):
                        evac(Ec[:, 8 * i:8 * i + 8, 0:15], pb[:, i * 128:i * 128 + 120].rearrange("p (a b) -> p a b", b=15), [pbB], [EcB])
                    pb, pbB = bank()
                    K.mm(lambda: T.transpose(out=pb[0:15, 0:128], in_=Uc[:, LU - 15:LU], identity=ident_f[:]), reads=[UcB, identfB], writes=[pbB])
                    evac(npp[0][:, 0:128], pb[0:15, 0:128], [pbB], [npp[1]])
                    K.dma(sp, o_npp[:, c * 128:(c + 1) * 128], npp[0][:], reads=[npp[1]], is_output=True)
                    K.op(dve, lambda: V.tensor_copy(out=usc[0][:].rearrange("p (a b) -> p a b", b=8), in_=Ec[:, :, 15:23]), reads=[EcB], writes=[usc[1]])
                    pb, pbB = bank()
                    K.mm(lambda: T.transpose(out=pb[:, 0:128], in_=usc[0][:], identity=ident_f[:]), reads=[usc[1], identfB], writes=[pbB])
                    evac(nps[0][:, 0:128], pb[:, 0:128], [pbB], [nps[1]])
                    K.dma(sp, npsd[:, c * 128:(c + 1) * 128], nps[0][:], reads=[nps[1]], writes=[npsdB])
                    src, srcB = Uc, UcB
                    ssrc, ssrcB = Ec, EcB
                    sh = 1
                    k = 0
                    while sh < w:
                        dst, dstB = (W1, W1B) if k % 2 == 0 else (W2, W2B)
                        sdst, sdstB = (V1, V1B) if k % 2 == 0 else (V2, V2B)
                        K.op(dve, lambda src=src, dst=dst, sh=sh: V.tensor_tensor(out=dst[:, 2 * sh - 1:LU], in0=src[:, 2 * sh - 1:LU], in1=src[:, sh - 1:LU - sh], op=ALU.add),
                             reads=[srcB], writes=[dstB])
                        K.op(dve, lambda ssrc=ssrc, sdst=sdst, sh=sh: V.tensor_tensor(out=sdst[:, :, 2 * sh - 1:23], in0=ssrc[:, :, 2 * sh - 1:23], in1=ssrc[:, :, sh - 1:23 - sh], op=ALU.add),
                             reads=[ssrcB], writes=[sdstB])
                        src, srcB, ssrc, ssrcB = dst, dstB, sdst, sdstB
                        sh *= 2
                        k += 1
                    Dc, DcB = Dd[j]
                    K.op(dve, lambda src=src: V.scalar_tensor_tensor(out=Dc[:, 0:HALF], in0=src[:, 16:LU], scalar=1.0 / w, in1=Uc[:, 16:LU], op0=ALU.mult, op1=ALU.subtract),
                         reads=[srcB, UcB], writes=[DcB])
                    K.op(dve, lambda src=src: V.tensor_tensor(out=src[:, 16:32], in0=src[:, 16:32], in1=inv_c[:, gi, :], op=ALU.mult),
                         reads=[srcB, invcB], writes=[srcB])
                    K.op(dve, lambda src=src: V.tensor_tensor(out=Dc[:, 0:16], in0=src[:, 16:32], in1=Uc[:, 16:32], op=ALU.subtract),
                         reads=[srcB, UcB], writes=[DcB])
                    K.op(dve, lambda ssrc=ssrc: V.scalar_tensor_tensor(out=Dc[:, HALF:HALF + 128].rearrange("p (a b) -> p a b", b=8), in0=ssrc[:, :, 15:23], scalar=1.0 / w,
                                                                         in1=Ec[:, :, 15:23], op0=ALU.mult, op1=ALU.subtract),
                         reads=[ssrcB, EcB], writes=[DcB])
                for oc in range(2):
                    c = 2 * gi + oc
                    for (c0, n) in split_groups(0, NMAIN):
                        pb, pbB = bank()
                        for k2 in range(2):
                            K.mm(lambda k2=k2: T.matmul(pb[:, 0:n], lhsT=wpl[:, gi, k2, oc * 128:(oc + 1) * 128], rhs=Dd[k2][0][:, c0:c0 + n],
                                                        start=(k2 == 0), stop=(k2 == 1)),
                                 reads=[wplB, Dd[k2][1]], writes=[pbB], inc=(k2 == 1))
                        K.op(dve, lambda: V.tensor_scalar(out=mxs[0][:, c0:c0 + n], in0=pb[:, 0:n], scalar1=vps[:, c:c + 1], scalar2=None, op0=ALU.mult),
                             reads=[pbB, vpsB], writes=[mxs[1]])
                    K.dma(sp, mixd[c], mxs[0][:], reads=[mxs[1]], writes=[mixB[c]])
            K.dma(sp, o_nps[:, 0:7, :], spool[:, 8:15, :], is_output=True)
            K.dma(sp, o_nps[:, 7:15, :], npsd.rearrange("(b t) c -> b t c", t=8), reads=[npsdB], is_output=True)

    def layer_norm(R, RB, t, res_rows_ap, g_row, b_row, tmps):
        XP, GB, st, mv = tmps
        PCS = 1024 if D % 1024 == 0 else 256
        for pc in range(D // PCS):
            cs = slice(pc * PCS, (pc + 1) * PCS)
            K.dma(sp, XP[0][:, 0:PCS], res_rows_ap[:, cs], writes=[XP[1]])
            K.op(dve, lambda cs=cs: V.scalar_tensor_tensor(out=R[:, t, cs], in0=XP[0][:, 0:PCS], scalar=cfg.ALPHA, in1=R[:, t, cs], op0=ALU.mult, op1=ALU.add),
                 reads=[XP[1], RB[t]], writes=[RB[t]])
        nch = D // 512 if D % 512 == 0 else D // 256
        csz = D // nch
        for i in range(nch):
            K.op(dve, lambda i=i: V.bn_stats(out=st[0][:, i, :], in_=R[:, t, i * csz:(i + 1) * csz]), reads=[RB[t]], writes=[st[1]])
        K.op(dve, lambda: V.bn_aggr(out=mv[0][:, 0:2], in_=st[0][:, 0:nch, :].rearrange("p a b -> p (a b)")), reads=[st[1]], writes=[mv[1]])
        rsqrt(mv[0][:, 2:3], mv[0][:, 1:2], 1e-5, [mv[1]], [mv[1]])
        K.op(dve, lambda: V.tensor_scalar(out=R[:, t, :], in0=R[:, t, :], scalar1=mv[0][:, 0:1], scalar2=mv[0][:, 2:3], op0=ALU.subtract, op1=ALU.mult),
             reads=[RB[t], mv[1]], writes=[RB[t]])
        for pc in range(D // PCS):
            cs = slice(pc * PCS, (pc + 1) * PCS)
            K.dma(sp, GB[0][:, 0, 0:PCS], v_ln[g_row:g_row + 1, cs].broadcast_to([128, PCS]), writes=[GB[1]])
            K.dma(sp, GB[0][:, 1, 0:PCS], v_ln[b_row:b_row + 1, cs].broadcast_to([128, PCS]), writes=[GB[1]])
            K.op(dve, lambda cs=cs: V.tensor_tensor(out=R[:, t, cs], in0=R[:, t, cs], in1=GB[0][:, 0, 0:PCS], op=ALU.mult), reads=[RB[t], GB[1]], writes=[RB[t]])
            K.op(dve, lambda cs=cs: V.tensor_tensor(out=R[:, t, cs], in0=R[:, t, cs], in1=GB[0][:, 1, 0:PCS], op=ALU.add), reads=[RB[t], GB[1]], writes=[RB[t]])

    def post_group(g):
        g0 = g * 384
        with ExitStack() as es:
            R, _ = K.sb("R", [128, 3, D], F32, es)
            RB = [K.buf() for _ in range(3)]
            hT, hTB = K.sb("hT", [128, KC, 384], BF16, es)
            XP = K.sb("XP", [128, 1024], F32, es)
            GB = K.sb("GB", [128, 2, 1024], F32, es)
            st = K.sb("st", [128, 16, 6], F32, es)
            mv = K.sb("mv", [128, 4], F32, es)
            tmps = (XP, GB, st, mv)
            slabs = [K.sb(f"pslab{i}", [128, cfg.SLAB], BF16, es) for i in range(2)]
            with ExitStack() as es2:
                MXg, MXgB = K.sb("MXg", [128, KC, 384], BF16, es2)
                for k0 in range(0, KC, 4):
                    kn = min(4, KC - k0)
                    K.dma(sp, MXg[:, k0:k0 + kn, :], mixd[k0:k0 + kn, :, g0:g0 + 384].rearrange("k p n -> p k n"), reads=mixB, writes=[MXgB])
                stream = SlabStream(K, slabs, [(w_o[i], KC * 256) for i in range(D // 256)])
                for s in range(D // 256):
                    slab, slabB = stream.next()
                    sv = slab[:, 0:KC * 256].rearrange("p (k n) -> p k n", n=256)
                    for t in range(3):
                        pb, pbB = bank()
                        for kc in range(KC):
                            K.mm(lambda kc=kc: T.matmul(pb[:, 0:256], lhsT=MXg[:, kc, t * 128:(t + 1) * 128], rhs=sv[:, kc, :],
                                                        start=(kc == 0), stop=(kc == KC - 1)),
                                 reads=[MXgB, slabB], writes=[pbB], inc=(kc == KC - 1))
                        evac(R[:, t, s * 256:(s + 1) * 256], pb[:, 0:256], [pbB], [RB[t]])
            K.barrier()
            for t in range(3):
                r0 = HALF + g0 + t * 128
                layer_norm(R, RB, t, xs[r0:r0 + 128, :], 0, 1, tmps)
                ti = g * 3 + t
                K.dma(sp, hsd[g0 + t * 128:g0 + (t + 1) * 128, :], R[:, t, :], reads=[RB[t]], writes=[hsB[ti]])
                transpose_rows(hT, hTB, t * 128, R[:, t, :], RB[t], 128)
            with ExitStack() as es2:
                AT_, ATB = K.sb("ACT_T", [128, FC, 384], BF16, es2)
                SG, SGB = K.sb("SG", [128, 384], F32, es2)
                stream = SlabStream(K, slabs, [(w_gu[i], KC * 256) for i in range(FC)])
                for f in range(FC):
                    slab, slabB = stream.next()
                    sv = slab[:, 0:KC * 256].rearrange("p (k n) -> p k n", n=256)
                    pg, pgB = bank()
                    pu, puB = bank()
                    for kc in range(KC):
                        K.mm(lambda kc=kc: T.matmul(pg[:, 0:384], lhsT=sv[:, kc, 0:128], rhs=hT[:, kc, :], start=(kc == 0), stop=(kc == KC - 1)),
                             reads=[slabB, hTB], writes=[pgB], inc=(kc == KC - 1))
                    for kc in range(KC):
                        K.mm(lambda kc=kc: T.matmul(pu[:, 0:384], lhsT=sv[:, kc, 128:256], rhs=hT[:, kc, :], start=(kc == 0), stop=(kc == KC - 1)),
                             reads=[slabB, hTB], writes=[puB], inc=(kc == KC - 1))
                    K.op(act, lambda: A.activation(out=SG[:], in_=pg[:, 0:384], func=AF.Silu), reads=[pgB], writes=[SGB])
                    K.op(dve, lambda f=f: V.tensor_tensor(out=AT_[:, f, :], in0=SG[:], in1=pu[:, 0:384], op=ALU.mult), reads=[SGB, puB], writes=[ATB])
                kss = [(k0, min(16, FC - k0)) for k0 in range(0, FC, 16)]
                srcs = []
                for nb_ in range(D // 512):
                    for (k0, kn) in kss:
                        srcs.append((w_dn[nb_][:, k0 * 512:(k0 + kn) * 512], kn * 512))
                stream = SlabStream(K, slabs, srcs)
                for nb_ in range(D // 512):
                    pbs = [bank() for _ in range(3)]
                    for (k0, kn) in kss:
                        slab, slabB = stream.next()
                        sv = slab[:, 0:kn * 512].rearrange("p (k n) -> p k n", n=512)
                        for t in range(3):
                            for kk in range(kn):
                                f = k0 + kk
                                K.mm(lambda kk=kk, f=f, t=t: T.matmul(pbs[t][0][:, 0:512], lhsT=AT_[:, f, t * 128:(t + 1) * 128], rhs=sv[:, kk, :],
                                                                      start=(f == 0), stop=(f == FC - 1)),
                                     reads=[ATB, slabB], writes=[pbs[t][1]], inc=(kk == kn - 1))
                    for t in range(3):
                        evac(R[:, t, nb_ * 512:(nb_ + 1) * 512], pbs[t][0][:, 0:512], [pbs[t][1]], [RB[t]])
            K.barrier()
            for t in range(3):
                ti = g * 3 + t
                sp.wait(hsB[ti].w)
                layer_norm(R, RB, t, hsd[g0 + t * 128:g0 + (t + 1) * 128, :], 2, 3, tmps)
                transpose_rows(hT, hTB, t * 128, R[:, t, :], RB[t], 128)
            with ExitStack() as es2:
                pr, prB = K.sb("pr", [128, 256], F32, es2)
                pT, pTB = K.sb("pT", [128, 2, 384], BF16, es2)
                gate, gateB = K.sb("gate", [128, 256], F32, es2)
                ge, geB = K.sb("ge", [128, 256], F32, es2)
                for t in range(3):
                    K.dma(sp, pr[:], ps_in[g0 + t * 128:g0 + (t + 1) * 128, :], writes=[prB])
                    pb, pbB = bank()
                    for j in range(2):
                        K.mm(lambda j=j: T.transpose(out=pb[:, j * 128:(j + 1) * 128], in_=pr[:, j * 128:(j + 1) * 128], identity=ident_f[:]),
                             reads=[prB, identfB], writes=[pbB], inc=(j == 1))
                    evac(pT[:, :, t * 128:(t + 1) * 128], pb[:, 0:256].rearrange("p (a b) -> p a b", b=128), [pbB], [pTB])
                srcs = [(w_pp, 2 * D)] + [(w_pg[i], KC * 256) for i in range(D // 256)]
                wpp, wppB = K.sb("wpp", [128, 2, D], BF16, es2)
                K.dma(pool, wpp[:, 0, :], w_pp[:, 0:D], writes=[wppB])
                wppB.wx.append(K.dma(pool, wpp[:, 1, :], w_pp[:, D:2 * D]))
                stream = SlabStream(K, slabs, srcs[1:])
                for s in range(D // 256):
                    slab, slabB = stream.next()
                    sv = slab[:, 0:KC * 256].rearrange("p (k n) -> p k n", n=256)
                    for t in range(3):
                        pb, pbB = bank()
                        for kc in range(KC):
                            K.mm(lambda kc=kc: T.matmul(pb[:, 0:256], lhsT=hT[:, kc, t * 128:(t + 1) * 128], rhs=sv[:, kc, :],
                                                        start=(kc == 0), stop=(kc == KC - 1)),
                                 reads=[hTB, slabB], writes=[pbB], inc=(kc == KC - 1))
                        K.op(act, lambda: A.activation(out=gate[:], in_=pb[:, 0:256], func=AF.Sigmoid), reads=[pbB], writes=[gateB])
                        pe_, pe_B = bank()
                        for j in range(2):
                            K.mm(lambda j=j: T.matmul(pe_[:, 0:256], lhsT=pT[:, j, t * 128:(t + 1) * 128], rhs=wpp[:, j, s * 256:(s + 1) * 256],
                                                      start=(j == 0), stop=(j == 1)),
                                 reads=[pTB, wppB], writes=[pe_B], inc=(j == 1))
                        K.op(dve, lambda: V.tensor_tensor(out=ge[:], in0=gate[:], in1=pe_[:, 0:256], op=ALU.mult), reads=[gateB, pe_B], writes=[geB])
                        K.op(dve, lambda s=s, t=t: V.tensor_tensor(out=R[:, t, s * 256:(s + 1) * 256], in0=R[:, t, s * 256:(s + 1) * 256], in1=ge[:], op=ALU.add),
                             reads=[RB[t], geB], writes=[RB[t]])
                for t in range(3):
                    K.dma(sp, yo[g0 + t * 128:g0 + (t + 1) * 128, :], R[:, t, :], reads=[RB[t]], is_output=True)
            K.barrier()
        K.barrier()

    mixer_pass(False)
    if stage >= 2:
        mixer_pass(True)
    if stage >= 3:
        for g in range(cfg.NG):
            post_group(g)
    K.finish()
    import os
    if os.environ.get("KDBG"):
        print("ENG COUNTS", {e.name: (e.count, e.dma_n) for e in (K.pe, K.act, K.dve, K.pool, K.sp)}, flush=True)
    return nc


def _slab_cols(W, cols):
    Kd = W.shape[0]
    kc = Kd // 128
    sub = W[:, cols]
    return np.ascontiguousarray(sub.reshape(kc, 128, -1).transpose(1, 0, 2)).reshape(128, -1)


def make_masks(block):
    idx = np.arange(128)
    same = (idx[:, None] // block) == (idx[None, :] // block)
    k = idx[:, None]
    i = idx[None, :]
    m = np.zeros((128, 5, 128), np.float32)
    m[:, 0] = (same & (k <= i))
    m[:, 1] = (same & (k > i))
    m[:, 2] = same
    m[:, 3] = np.where(same & (i >= k), 0.0, -30000.0)
    m[:, 4] = np.where(same & (i > k), 0.0, -30000.0)
    return m


def prep_shared(cfg, inp):
    D, NH, KC, DN, PW, DFF, FC = cfg.D, cfg.NH, cfg.KC, cfg.DN, cfg.PW, cfg.DFF, cfg.FC
    f = np.float32
    w_in = np.asarray(inp["w_in"][0], f)
    o1 = PW
    sh = {}
    qo, ko, vo, zo = o1, o1 + DN, o1 + 2 * DN, o1 + 3 * DN
    bo = o1 + 4 * DN
    r = np.arange(128)
    for h in range(NH):
        sh[f"w_hd{h}"] = np.stack([_slab_cols(w_in, off + h * 128 + r) for off in (ko, vo, qo, zo)])
    sh["w_u"] = np.stack([_slab_cols(w_in, np.arange(i * 128, (i + 1) * 128)) for i in range(8)])
    sh["w_ba"] = _slab_cols(w_in, np.arange(bo, bo + 2 * NH))
    w_out = np.asarray(inp["w_out"][0], f)
    for i in range(D // 256):
        sh[f"w_o{i}"] = _slab_cols(w_out, np.arange(i * 256, (i + 1) * 256))
    wgu = np.asarray(inp["w_gate_up"][0], f)
    GUG = 8
    for g in range((FC + GUG - 1) // GUG):
        sh[f"w_gu{g}"] = np.stack([_slab_cols(wgu, np.concatenate([np.arange(i * 128, (i + 1) * 128), DFF + np.arange(i * 128, (i + 1) * 128)]))
                                   for i in range(g * GUG, min(FC, (g + 1) * GUG))])
    wd = np.asarray(inp["w_down"][0], f)
    for i in range(D // 512):
        sh[f"w_dn{i}"] = _slab_cols(wd, np.arange(i * 512, (i + 1) * 512))
    wpg = np.asarray(inp["w_ple_gate"][0], f)
    for i in range(D // 256):
        sh[f"w_pg{i}"] = _slab_cols(wpg, np.arange(i * 256, (i + 1) * 256))
    sh["w_pp"] = _slab_cols(np.asarray(inp["w_ple_proj"][0], f), np.arange(D))
    wp = np.asarray(inp["w_pool"][0], f)
    sh["w_pl"] = np.ascontiguousarray(wp.reshape(4, 2, 128, 256).transpose(2, 0, 1, 3)).reshape(128, -1)
    sh["v_ps"] = np.ascontiguousarray(np.asarray(inp["pool_scale"][0], f).reshape(PW // 128, 128).T)
    wc = np.asarray(inp["w_conv"][0], f)
    sh["v_wc"] = np.ascontiguousarray(wc.reshape(4, 3 * NH, 128).transpose(2, 1, 0)).reshape(128, -1)
    hb = np.concatenate([np.asarray(inp["dt_bias"][0], f), np.asarray(inp["a_log"][0], f)])
    sh["v_hb"] = np.ascontiguousarray(np.broadcast_to(hb[None, :], (128, 2 * NH)))
    sh["v_on"] = np.ascontiguousarray(np.asarray(inp["o_norm_g"][0], f).reshape(128, 1))
    sh["v_ln"] = np.stack([np.asarray(inp[k][0], f) for k in ("ln1_g", "ln1_b", "ln2_g", "ln2_b")])
    sh["cm_p"] = make_masks(128)
    sh["cm_s"] = make_masks(8)
    sh["c_id"] = np.eye(128, dtype=f)
    idx = np.arange(128)
    bmk = np.zeros((128, 16, 128), f)
    for b in range(16):
        bmk[:, b, 8 * b:8 * b + 8] = 1.0
    sh["c_bm"] = bmk
    sh["c_rm"] = ((idx[:, None] // 8) == np.arange(16)[None, :]).astype(f)
    return sh


def prep_core(cfg, inp, c):
    f = np.float32
    HALF, D = cfg.HALF, cfg.D
    p, half = c // 2, c % 2
    xp = np.asarray(inp["x_prompt"], f)
    xsam = np.asarray(inp["x_sample"], f)
    npc = cfg.DECB // 8
    assert npc == 16
    xs = np.zeros((2 * HALF + 128, D), f)
    if half == 1:
        xs[0:HALF] = xp[p, 0:HALF]
    xs[HALF:2 * HALF] = xp[p, half * HALF:(half + 1) * HALF]
    xs[2 * HALF:] = xsam[16 * c:16 * c + 16].reshape(128, D)
    m = {"xs": xs}
    pp = np.asarray(inp["p_prompt"][0], f)
    psm = np.asarray(inp["p_sample"][0], f)
    m["ps"] = np.concatenate([pp[p, half * HALF:(half + 1) * HALF], psm[16 * c:16 * c + 16].reshape(128, -1)], 0)
    m["spool"] = np.ascontiguousarray(np.asarray(inp["state_pool"][0], f)[16 * c:16 * c + 16])
    sc = np.asarray(inp["state_conv"][0], f)[16 * c:16 * c + 16]
    m["sconv"] = np.ascontiguousarray(sc.reshape(48, 3, cfg.NH, 128))
    m["sdelta"] = np.ascontiguousarray(np.asarray(inp["state_delta"][0], f)[16 * c:16 * c + 16])
    invc = np.zeros((128, 4, 16), f)
    for gi in range(4):
        w = 2 ** (gi + 1)
        pos = np.arange(16) + half * HALF
        invc[:, gi, :] = (1.0 / np.minimum(pos + 1, w))[None, :]
    m["invc"] = invc
    return m


_CACHE = {}


def run(cfg, inp):
    import time
    t0 = time.time()
    if "nc" not in _CACHE:
        _CACHE["nc"] = build(cfg)
    nc = _CACHE["nc"]
    sh = prep_shared(cfg, inp)
    in_maps = []
    for c in range(8):
        m = dict(sh)
        m.update(prep_core(cfg, inp, c))
        in_maps.append(m)
    print("[kernel] prep done %.1fs" % (time.time() - t0), flush=True)
    res = run_bass_kernel_spmd(nc, in_maps, core_ids=list(range(8)))
    print("[kernel] launch done %.1fs" % (time.time() - t0), flush=True)
    R = res.results
    HALF, D, NH = cfg.HALF, cfg.D, cfg.NH
    f = np.float32
    yp = np.zeros((cfg.BATCH, cfg.SEQ, D), f)
    ys = np.zeros((cfg.DECB, 8, D), f)
    npp = np.zeros((1, cfg.BATCH, 15, 1024), f)
    ncp = np.zeros((1, cfg.BATCH, 3, 3 * cfg.DN), f)
    ndp = np.zeros((1, cfg.BATCH, NH, 128, 128), f)
    nps = np.zeros((1, cfg.DECB, 15, 1024), f)
    ncs = np.zeros((1, cfg.DECB, 3, 3 * cfg.DN), f)
    nds = np.zeros((1, cfg.DECB, NH, 128, 128), f)
    for c in range(8):
        p, half = c // 2, c % 2
        r = R[c]
        yp[p, half * HALF:(half + 1) * HALF] = r["yo"][0:HALF]
        ys[16 * c:16 * c + 16] = r["yo"][HALF:].reshape(16, 8, D)
        if half == 1:
            npp[0, p] = r["o_npp"]
            ncp[0, p] = r["o_ncp"].reshape(3, -1)
            ndp[0, p] = r["o_ndp"]
        nps[0, 16 * c:16 * c + 16] = r["o_nps"]
        ncs[0, 16 * c:16 * c + 16] = r["o_ncs"].reshape(16, 3, -1)
        nds[0, 16 * c:16 * c + 16] = r["o_nds"]
    return (yp, ys, npp, ncp, ndp, nps, ncs, nds)


def kernel(**inputs):
    cfg = Cfg()
    return run(cfg, inputs)
```
